# Optimizing a Trainium2 kernel written in Bass

```python
import math
import jax
import jax.numpy as jnp
from jax import lax
import numpy as np

D_MODEL = 1024
BATCH = 4
SEQ = 4096
DEPTH = 4

CHUNK = 64
Q_BLOCK = 128
D_FF = 2816
RMS_EPS = 1e-6
NEG_INF = -1e30

DIFF_HEADS = 4
DIFF_QK_DIM = 64
DIFF_V_DIM = 2 * DIFF_QK_DIM
SB_HEADS = 4
SB_HEAD_DIM = 128
DIFF_Q_COLS = DIFF_HEADS * 2 * DIFF_QK_DIM
DIFF_V_COLS = DIFF_HEADS * DIFF_V_DIM
SB_COLS = SB_HEADS * SB_HEAD_DIM
ATT_IN_DIM = 2 * DIFF_Q_COLS + DIFF_V_COLS + 3 * SB_COLS
ATT_MIX_DIM = DIFF_V_COLS + SB_COLS

GDN_HEADS = 8
GDN_HEAD_DIM = 128
GDN_CONV = 4
GDN_MIX_DIM = GDN_HEADS * GDN_HEAD_DIM
GDN_QKV_DIM = 3 * GDN_MIX_DIM
GDN_IN_DIM = GDN_QKV_DIM + GDN_MIX_DIM + 2 * GDN_HEADS

N_EVEN = (DEPTH + 1) // 2
N_ODD = DEPTH // 2

kernel_name = 'hybrid_streaming_diff_sb_gdn'


def rms_norm(x, w):
    x32 = x.astype(jnp.float32)
    y = x32 * lax.rsqrt(jnp.mean(x32 * x32, axis=-1, keepdims=True) + RMS_EPS)
    return (y * w.astype(jnp.float32)).astype(x.dtype)


def swiglu(h, w_gate, w_up, w_down):
    return (jax.nn.silu(h @ w_gate) * (h @ w_up)) @ w_down


def l2_normalize(x):
    return x * lax.rsqrt(jnp.sum(x * x, axis=-1, keepdims=True) + 1e-6)


def diff_attention(q, k, v, lam, slopes):
    b, s_len, h, _, _ = q.shape
    scale = DIFF_QK_DIM ** -0.5
    pos_k = jnp.arange(s_len)

    def block(i):
        start = i * Q_BLOCK
        qb = lax.dynamic_slice_in_dim(q, start, Q_BLOCK, axis=1)
        pos_q = start + jnp.arange(Q_BLOCK)
        dist = jnp.abs(pos_q[:, None] - pos_k[None, :]).astype(jnp.float32)
        allowed = (pos_k[None, :] // CHUNK) <= (pos_q[:, None] // CHUNK)
        bias = jnp.where(allowed, -slopes[:, None, None] * dist, NEG_INF)
        scores = jnp.einsum('bqhcd,bkhcd->bchqk', qb, k) * scale + bias
        p = jax.nn.softmax(scores, axis=-1)
        weights = p[:, 0] - lam * p[:, 1]
        return jnp.einsum('bhqk,bkhd->bqhd', weights, v)

    out = lax.map(block, jnp.arange(s_len // Q_BLOCK))
    return out.transpose(1, 0, 2, 3, 4).reshape(b, s_len, h, v.shape[-1])


def stick_breaking_attention(q, k, v):
    b, s_len, h, d = q.shape
    scale = SB_HEAD_DIM ** -0.5
    pos_k = jnp.arange(s_len)

    def block(i):
        start = i * Q_BLOCK
        qb = lax.dynamic_slice_in_dim(q, start, Q_BLOCK, axis=1)
        pos_q = start + jnp.arange(Q_BLOCK)
        earlier = pos_k[None, :] < pos_q[:, None]
        z = jnp.einsum('bqhd,bkhd->bhqk', qb, k) * scale
        log_beta = jax.nn.log_sigmoid(z)
        log_keep = jnp.where(earlier, jax.nn.log_sigmoid(-z), 0.0)
        log_after = lax.cumsum(log_keep, axis=3, reverse=True) - log_keep
        a = jnp.where(earlier, jnp.exp(log_beta + log_after), 0.0)
        return jnp.einsum('bhqk,bkhd->bqhd', a, v)

    out = lax.map(block, jnp.arange(s_len // Q_BLOCK))
    return out.transpose(1, 0, 2, 3, 4).reshape(b, s_len, h, d)


def causal_depthwise_conv(x, w):
    taps = w.shape[0]
    s_len = x.shape[1]
    xp = jnp.pad(x, ((0, 0), (taps - 1, 0), (0, 0)))
    y = xp[:, 0:s_len] * w[0]
    for i in range(1, taps):
        y = y + xp[:, i:i + s_len] * w[i]
    return y


def gated_delta_rule_chunked(q, k, v, g, beta):
    b, s_len, h, dk = q.shape
    dv = v.shape[-1]
    n_chunks = s_len // CHUNK

    def to_chunks(t):
        t = t.reshape((b, n_chunks, CHUNK, h) + t.shape[3:])
        return jnp.moveaxis(t, 3, 1)

    q = to_chunks(q * dk ** -0.5)
    k = to_chunks(k)
    v = to_chunks(v)
    g = to_chunks(g)
    beta = to_chunks(beta)
    gc = jnp.cumsum(g, axis=-1)
    incl = jnp.tril(jnp.ones((CHUNK, CHUNK), dtype=bool))
    strict = jnp.tril(jnp.ones((CHUNK, CHUNK), dtype=bool), -1)
    decay = jnp.exp(jnp.where(incl, gc[..., :, None] - gc[..., None, :], -jnp.inf))
    kb = k * beta[..., None]
    vb = v * beta[..., None]
    l_mat = jnp.where(strict, jnp.einsum('bhncd,bhnsd->bhncs', kb, k) * decay, 0.0)
    eye = jnp.eye(CHUNK, dtype=q.dtype)
    t_mat = lax.linalg.triangular_solve(l_mat + eye, jnp.broadcast_to(eye, l_mat.shape),
                                        left_side=True, lower=True, unit_diagonal=True)
    u = jnp.einsum('bhncs,bhnsd->bhncd', t_mat, vb)
    w = jnp.einsum('bhncs,bhnsd->bhncd', t_mat, kb * jnp.exp(gc)[..., None])
    a_qk = jnp.where(incl, jnp.einsum('bhncd,bhnsd->bhncs', q, k) * decay, 0.0)
    q_dec = q * jnp.exp(gc)[..., None]
    k_dec = k * jnp.exp(gc[..., -1:] - gc)[..., None]
    chunk_decay = jnp.exp(gc[..., -1])
    xs = (jnp.moveaxis(q_dec, 2, 0), jnp.moveaxis(k_dec, 2, 0), jnp.moveaxis(u, 2, 0),
          jnp.moveaxis(w, 2, 0), jnp.moveaxis(a_qk, 2, 0), jnp.moveaxis(chunk_decay, 2, 0))

    def step(state, inp):
        qd, kd, un, wn, aqk, cd = inp
        v_new = un - jnp.einsum('bhcd,bhde->bhce', wn, state)
        o = jnp.einsum('bhcd,bhde->bhce', qd, state) + jnp.einsum('bhcs,bhse->bhce', aqk, v_new)
        state = state * cd[..., None, None] + jnp.einsum('bhcd,bhce->bhde', kd, v_new)
        return state, o

    state0 = jnp.zeros((b, h, dk, dv), dtype=q.dtype)
    _, out = lax.scan(step, state0, xs)
    return out.transpose(1, 0, 3, 2, 4).reshape(b, s_len, h, dv)


def even_mixer(h, w_in, diff_lambda, diff_subln, w_out, layer_idx):
    b, s_len, _ = h.shape
    proj = (h @ w_in).astype(jnp.float32)
    splits = [DIFF_Q_COLS, 2 * DIFF_Q_COLS, 2 * DIFF_Q_COLS + DIFF_V_COLS,
              2 * DIFF_Q_COLS + DIFF_V_COLS + SB_COLS, 2 * DIFF_Q_COLS + DIFF_V_COLS + 2 * SB_COLS]
    qa, ka, va, qs, ks, vs = jnp.split(proj, splits, axis=-1)
    qa = qa.reshape(b, s_len, DIFF_HEADS, 2, DIFF_QK_DIM)
    ka = ka.reshape(b, s_len, DIFF_HEADS, 2, DIFF_QK_DIM)
    va = va.reshape(b, s_len, DIFF_HEADS, DIFF_V_DIM)
    qs = qs.reshape(b, s_len, SB_HEADS, SB_HEAD_DIM)
    ks = ks.reshape(b, s_len, SB_HEADS, SB_HEAD_DIM)
    vs = vs.reshape(b, s_len, SB_HEADS, SB_HEAD_DIM)
    lambda_init = 0.8 - 0.6 * math.exp(-0.3 * layer_idx)
    lp = diff_lambda.astype(jnp.float32)
    lam = jnp.exp(jnp.sum(lp[0] * lp[1])) - jnp.exp(jnp.sum(lp[2] * lp[3])) + lambda_init
    slopes = 2.0 ** (-8.0 * jnp.arange(1, DIFF_HEADS + 1, dtype=jnp.float32) / DIFF_HEADS)
    oa = diff_attention(qa, ka, va, lam, slopes)
    oa = rms_norm(oa, diff_subln) * (1.0 - lambda_init)
    osb = stick_breaking_attention(qs, ks, vs)
    o = jnp.concatenate([oa.reshape(b, s_len, DIFF_V_COLS), osb.reshape(b, s_len, SB_COLS)], axis=-1)
    return o.astype(h.dtype) @ w_out


def odd_mixer(h, w_in, conv_w, a_log, dt_bias, norm_w, w_out):
    b, s_len, _ = h.shape
    proj = h @ w_in
    qkv, gate, b_raw, a_raw = jnp.split(
        proj, [GDN_QKV_DIM, GDN_QKV_DIM + GDN_MIX_DIM, GDN_QKV_DIM + GDN_MIX_DIM + GDN_HEADS], axis=-1)
    qkv = jax.nn.silu(causal_depthwise_conv(qkv, conv_w)).astype(jnp.float32)
    q, k, v = jnp.split(qkv, 3, axis=-1)
    q = l2_normalize(q.reshape(b, s_len, GDN_HEADS, GDN_HEAD_DIM))
    k = l2_normalize(k.reshape(b, s_len, GDN_HEADS, GDN_HEAD_DIM))
    v = v.reshape(b, s_len, GDN_HEADS, GDN_HEAD_DIM)
    beta = jax.nn.sigmoid(b_raw.astype(jnp.float32))
    g = -jnp.exp(a_log.astype(jnp.float32)) * jax.nn.softplus(
        a_raw.astype(jnp.float32) + dt_bias.astype(jnp.float32))
    o = gated_delta_rule_chunked(q, k, v, g, beta)
    o = rms_norm(o, norm_w) * jax.nn.silu(gate.astype(jnp.float32).reshape(b, s_len, GDN_HEADS, GDN_HEAD_DIM))
    return o.reshape(b, s_len, GDN_MIX_DIM).astype(h.dtype) @ w_out


def setup_inputs(seed: int = 0) -> dict:
    key = jax.random.key(seed)
    ks = jax.random.split(key, 24)
    f32 = jnp.float32

    def dense(k, shape, fan_in):
        return jax.random.normal(k, shape, f32) * fan_in ** -0.5

    def gain(k, shape):
        return 1.0 + 0.02 * jax.random.normal(k, shape, f32)

    x = jax.random.normal(ks[0], (BATCH, SEQ, D_MODEL), f32)
    ffn1_norm = gain(ks[1], (DEPTH, D_MODEL))
    ffn1_w_gate = dense(ks[2], (DEPTH, D_MODEL, D_FF), D_MODEL)
    ffn1_w_up = dense(ks[3], (DEPTH, D_MODEL, D_FF), D_MODEL)
    ffn1_w_down = dense(ks[4], (DEPTH, D_FF, D_MODEL), D_FF)
    mix_norm = gain(ks[5], (DEPTH, D_MODEL))
    att_w_in = dense(ks[6], (N_EVEN, D_MODEL, ATT_IN_DIM), D_MODEL)
    diff_lambda = 0.1 * jax.random.normal(ks[7], (N_EVEN, 4, DIFF_QK_DIM), f32)
    diff_subln = gain(ks[8], (N_EVEN, DIFF_V_DIM))
    att_w_out = dense(ks[9], (N_EVEN, ATT_MIX_DIM, D_MODEL), ATT_MIX_DIM)
    gdn_w_in = dense(ks[10], (N_ODD, D_MODEL, GDN_IN_DIM), D_MODEL)
    gdn_conv_w = dense(ks[11], (N_ODD, GDN_CONV, GDN_QKV_DIM), GDN_CONV)
    gdn_a_log = jnp.log(jax.random.uniform(ks[12], (N_ODD, GDN_HEADS), f32, 1.0, 16.0))
    dt = jnp.exp(jax.random.uniform(ks[13], (N_ODD, GDN_HEADS), f32, math.log(1e-3), math.log(1e-1)))
    gdn_dt_bias = dt + jnp.log(-jnp.expm1(-dt))
    gdn_norm = gain(ks[14], (N_ODD, GDN_HEAD_DIM))
    gdn_w_out = dense(ks[15], (N_ODD, GDN_MIX_DIM, D_MODEL), GDN_MIX_DIM)
    ffn2_norm = gain(ks[16], (DEPTH, D_MODEL))
    ffn2_w_gate = dense(ks[17], (DEPTH, D_MODEL, D_FF), D_MODEL)
    ffn2_w_up = dense(ks[18], (DEPTH, D_MODEL, D_FF), D_MODEL)
    ffn2_w_down = dense(ks[19], (DEPTH, D_FF, D_MODEL), D_FF)
    final_norm = gain(ks[20], (D_MODEL,))
    return {'x': x, 'ffn1_norm': ffn1_norm, 'ffn1_w_gate': ffn1_w_gate, 'ffn1_w_up': ffn1_w_up,
            'ffn1_w_down': ffn1_w_down, 'mix_norm': mix_norm, 'att_w_in': att_w_in,
            'diff_lambda': diff_lambda, 'diff_subln': diff_subln, 'att_w_out': att_w_out,
            'gdn_w_in': gdn_w_in, 'gdn_conv_w': gdn_conv_w, 'gdn_a_log': gdn_a_log,
            'gdn_dt_bias': gdn_dt_bias, 'gdn_norm': gdn_norm, 'gdn_w_out': gdn_w_out,
            'ffn2_norm': ffn2_norm, 'ffn2_w_gate': ffn2_w_gate, 'ffn2_w_up': ffn2_w_up,
            'ffn2_w_down': ffn2_w_down, 'final_norm': final_norm}


def reference(x, ffn1_norm, ffn1_w_gate, ffn1_w_up, ffn1_w_down, mix_norm, att_w_in,
              diff_lambda, diff_subln, att_w_out, gdn_w_in, gdn_conv_w, gdn_a_log,
              gdn_dt_bias, gdn_norm, gdn_w_out, ffn2_norm, ffn2_w_gate, ffn2_w_up,
              ffn2_w_down, final_norm):
    for layer in range(DEPTH):
        x = x + 0.5 * swiglu(rms_norm(x, ffn1_norm[layer]), ffn1_w_gate[layer],
                             ffn1_w_up[layer], ffn1_w_down[layer])
        h = rms_norm(x, mix_norm[layer])
        if layer % 2 == 0:
            e = layer // 2
            x = x + even_mixer(h, att_w_in[e], diff_lambda[e], diff_subln[e], att_w_out[e], layer)
        else:
            o = layer // 2
            x = x + odd_mixer(h, gdn_w_in[o], gdn_conv_w[o], gdn_a_log[o], gdn_dt_bias[o],
                              gdn_norm[o], gdn_w_out[o])
        x = x + 0.5 * swiglu(rms_norm(x, ffn2_norm[layer]), ffn2_w_gate[layer],
                             ffn2_w_up[layer], ffn2_w_down[layer])
    return rms_norm(x, final_norm)
```

```python
import numpy as np
import ml_dtypes
from contextlib import ExitStack
import concourse.bass as bass
import concourse.mybir as mybir
from concourse.bass_utils import run_bass_kernel_spmd

F32 = mybir.dt.float32
BF16 = mybir.dt.bfloat16
AF = mybir.ActivationFunctionType
ALU = mybir.AluOpType
AX = mybir.AxisListType

D = 1024
DFF = 2816
SEQ = 4096
NB = 4
DEPTH = 4
TOK = 2048
NT = TOK // 128
EPS = 1e-6
NCORES = 8


class _Ctr:
    def __init__(self, S, name, step):
        self.S = S
        self.name = name
        self.step = step
        self.n = 0
        self.sem = None
        self.val = 0
        self._new()

    def _new(self):
        self.sem = self.S.new_sem("%s_%d" % (self.name, self.n))
        self.n += 1
        self.val = 0

    def next(self):
        if self.val + self.step > 30000:
            self._new()
        self.val += self.step
        return (self.sem, self.val)

    def last(self):
        if self.val == 0:
            return None
        return (self.sem, self.val)


class Sched:
    def __init__(self, nc, es):
        self.nc = nc
        self.es = es
        self.eng = {"pe": nc.tensor, "dve": nc.vector, "act": nc.scalar, "pool": nc.gpsimd, "sp": nc.sync}
        self.nsem = 0
        self.ctr = {e: _Ctr(self, e, 1) for e in ("pe", "dve", "act", "pool")}
        self.dctr = {}
        self.known = {e: {} for e in self.eng}
        self.state = {}
        self.nwaits = 0
        self.nops = 0

    def new_sem(self, name):
        self.nsem += 1
        return self.es.enter_context(self.nc.semaphore("s%d_%s" % (self.nsem, name)))

    @staticmethod
    def _merge(d, tok):
        if tok is None:
            return
        sem, val = tok
        k = id(sem)
        if k not in d or d[k][1] < val:
            d[k] = (sem, val)

    def _deps(self, reads, writes):
        need = {}
        for r in reads:
            st = self.state.get(r)
            if st is not None:
                for t in st[0].values():
                    self._merge(need, t)
        for w in writes:
            st = self.state.get(w)
            if st is not None:
                for t in st[0].values():
                    self._merge(need, t)
                for t in st[1].values():
                    self._merge(need, t)
        return need

    def _emit_waits(self, e, need):
        kn = self.known[e]
        eng = self.eng[e]
        for k, (sem, val) in need.items():
            if kn.get(k, 0) >= val:
                continue
            eng.wait_ge(sem, val)
            kn[k] = val
            self.nwaits += 1

    def _record(self, tok, reads, writes):
        for r in reads:
            st = self.state.setdefault(r, [{}, {}])
            self._merge(st[1], tok)
        for w in writes:
            self.state[w] = [{}, {}]
            self._merge(self.state[w][0], tok)

    def op(self, e, fns, reads=(), writes=()):
        if not isinstance(fns, (list, tuple)):
            fns = [fns]
        need = self._deps(reads, writes)
        self._emit_waits(e, need)
        eng = self.eng[e]
        ins = None
        for f in fns:
            ins = f(eng)
            self.nops += 1
        tok = self.ctr[e].next()
        ins.then_inc(tok[0], 1)
        self._record(tok, reads, writes)
        return tok

    def dma(self, e, dkey, fns, reads=(), writes=()):
        if not isinstance(fns, (list, tuple)):
            fns = [fns]
        need = self._deps(reads, writes)
        self._emit_waits(e, need)
        eng = self.eng[e]
        c = self.dctr.get(dkey)
        if c is None:
            c = self.dctr[dkey] = _Ctr(self, "d" + dkey, 16)
        tok = None
        for f in fns:
            tok = c.next()
            f(eng).then_inc(tok[0], 16)
            self.nops += 1
        self._record(tok, reads, writes)
        return tok

    def barrier(self):
        toks = {}
        for c in list(self.ctr.values()) + list(self.dctr.values()):
            self._merge(toks, c.last())
        for e in self.eng:
            self._emit_waits(e, dict(toks))
        self.state = {}

    def finish(self, e="sp"):
        toks = {}
        for c in list(self.ctr.values()) + list(self.dctr.values()):
            self._merge(toks, c.last())
        self._emit_waits(e, toks)


class Ctx:
    def __init__(self, name="k"):
        self.nc = bass.Bass("TRN2", target_bir_lowering=False)
        self.es = ExitStack()
        self.S = Sched(self.nc, self.es)
        self.inputs = []
        self.outputs = []
        self.n = 0

    def dram_in(self, name, shape, dt):
        self.inputs.append(name)
        return self.nc.dram_tensor(name, list(shape), dt, kind="ExternalInput").ap()

    def dram_out(self, name, shape, dt):
        self.outputs.append(name)
        return self.nc.dram_tensor(name, list(shape), dt, kind="ExternalOutput").ap()

    def sb(self, es, name, shape, dt):
        self.n += 1
        return es.enter_context(self.nc.sbuf_tensor("%s_%d" % (name, self.n), list(shape), dt))

    def ps(self, es, name, shape, dt):
        self.n += 1
        return es.enter_context(self.nc.psum_tensor("%s_%d" % (name, self.n), list(shape), dt))

    def close(self):
        self.S.finish("sp")
        self.es.close()


def bcast_rows(ap1d, nparts=128):
    return ap1d.rearrange("(o n) -> o n", o=1).broadcast_to([nparts, ap1d.shape[0]])


class TokPhase:
    def __init__(self, C, es, ident_ap):
        self.C = C
        S = C.S
        self.x = C.sb(es, "x", [128, NT, D], F32)
        self.hT = C.sb(es, "hT", [128, 8, 1024], BF16)
        self.hb = [C.sb(es, "hb", [128, D], BF16) for _ in range(2)]
        self.sq = C.sb(es, "sq", [128, D], F32)
        self.ss = C.sb(es, "ss", [128, NT], F32)
        self.rstd = C.sb(es, "rstd", [128, NT], F32)
        self.wbc = C.sb(es, "wbc", [128, D], F32)
        self.ident = C.sb(es, "ident", [128, 128], BF16)
        self.psum = C.ps(es, "psum", [128, 8, 512], F32)
        S.dma("sp", "ident", lambda e: e.dma_start(out=self.ident[:], in_=ident_ap), writes=["ident"])

    def load_x(self, x_ap):
        S = self.C.S
        S.dma("sp", "xld", [lambda e, t=t: e.dma_start(out=self.x[:, t, :], in_=x_ap[t * 128:(t + 1) * 128, :])
                            for t in range(NT)], writes=[("x", t) for t in range(NT)])

    def store_x(self, out_ap):
        S = self.C.S
        S.dma("sp", "xst", [lambda e, t=t: e.dma_start(out=out_ap[t * 128:(t + 1) * 128, :], in_=self.x[:, t, :])
                            for t in range(NT)], reads=[("x", t) for t in range(NT)], writes=["xout"])

    def norm_stats(self, tiles):
        S = self.C.S
        for t in tiles:
            S.op("act", lambda e, t=t: e.activation(out=self.sq[:], in_=self.x[:, t, :], func=AF.Square,
                                                     scale=1.0 / 32.0, accum_out=self.ss[:, t:t + 1]),
                 reads=[("x", t)], writes=["sq", ("ss", t)])
        t0, t1 = tiles[0], tiles[-1] + 1
        S.op("dve", lambda e: e.tensor_scalar(out=self.rstd[:, t0:t1], in0=self.ss[:, t0:t1], scalar1=EPS,
                                              scalar2=None, op0=ALU.add),
             reads=[("ss", t) for t in tiles], writes=[("rstd", t) for t in tiles])
        S.op("act", lambda e: e.activation(out=self.rstd[:, t0:t1], in_=self.rstd[:, t0:t1], func=AF.Ln),
             reads=[("rstd", t) for t in tiles], writes=[("rstd", t) for t in tiles])
        S.op("act", lambda e: e.activation(out=self.rstd[:, t0:t1], in_=self.rstd[:, t0:t1], func=AF.Exp, scale=-0.5),
             reads=[("rstd", t) for t in tiles], writes=[("rstd", t) for t in tiles])

    def load_normw(self, w_ap):
        S = self.C.S
        S.dma("sp", "wbc", lambda e: e.dma_start(out=self.wbc[:], in_=bcast_rows(w_ap)), writes=["wbc"])

    def norm_T(self, tiles, pbank=7):
        S = self.C.S
        for j, t in enumerate(tiles):
            hb = self.hb[j % 2]
            hk = ("hb", j % 2)
            S.op("dve", lambda e, t=t, hb=hb: e.scalar_tensor_tensor(
                out=hb[:], in0=self.x[:, t, :], scalar=self.rstd[:, t:t + 1], in1=self.wbc[:],
                op0=ALU.mult, op1=ALU.mult),
                reads=[("x", t), ("rstd", t), "wbc"], writes=[hk])
            pt = self.psum[:, pbank, :].bitcast(BF16)
            S.op("pe", [lambda e, k=k, hb=hb, pt=pt: e.transpose(out=pt[:, k * 128:(k + 1) * 128],
                                                                  in_=hb[:, k * 128:(k + 1) * 128],
                                                                  identity=self.ident[:]) for k in range(8)],
                 reads=[hk, "ident"], writes=[("ps", pbank)])
            S.op("act", lambda e, j=j, pt=pt: e.copy(out=self.hT[:, :, j * 128:(j + 1) * 128],
                                                      in_=pt.rearrange("p (k t) -> p k t", k=8)),
                 reads=[("ps", pbank)], writes=[("hT", j)])


def emit_ffn(C, es0, T, wn_ap, wg_ap, wu_ap, wd_ap, final_scale=0.5):
    S = C.S
    NCH = DFF // 128
    with ExitStack() as es:
        actT = C.sb(es, "actT", [128, NCH, 1024], BF16)
        wgb = [C.sb(es, "wgb", [128, 8, 512], BF16) for _ in range(2)]
        wub = [C.sb(es, "wub", [128, 8, 512], BF16) for _ in range(2)]
        wdb = [C.sb(es, "wdb", [128, 4, 512], BF16) for _ in range(3)]
        sg = [C.sb(es, "sg", [128, 512], F32) for _ in range(2)]
        T.load_normw(wn_ap)
        wg_v = wg_ap.rearrange("(k p) n -> p k n", p=128)
        wu_v = wu_ap.rearrange("(k p) n -> p k n", p=128)
        wd_v = wd_ap.rearrange("(c p) n -> p c n", p=128)
        nsc = (DFF + 511) // 512
        wcnt = 0
        dcnt = 0
        pcnt = 0
        for g in range(TOK // 1024):
            tiles = list(range(g * 8, g * 8 + 8))
            T.norm_stats(tiles)
            T.norm_T(tiles)
            for sc in range(nsc):
                c0 = sc * 512
                w = min(512, DFF - c0)
                b = wcnt % 2
                wcnt += 1
                S.dma("pool", "wg%d" % b, lambda e, b=b, c0=c0, w=w: e.dma_start(out=wgb[b][:, :, 0:w], in_=wg_v[:, :, c0:c0 + w]),
                      writes=[("wgb", b)])
                S.dma("pool", "wu%d" % b, lambda e, b=b, c0=c0, w=w: e.dma_start(out=wub[b][:, :, 0:w], in_=wu_v[:, :, c0:c0 + w]),
                      writes=[("wub", b)])
                for c4 in range(w // 128):
                    c = sc * 4 + c4
                    for th in range(2):
                        pg = (pcnt % 2) * 2
                        pu = pg + 1
                        sgi = pcnt % 2
                        pcnt += 1
                        hreads = [("hT", j) for j in range(th * 4, th * 4 + 4)]
                        S.op("pe", [lambda e, k=k, b=b, c4=c4, th=th, pg=pg: e.matmul(
                            T.psum[:, pg, :], lhsT=wgb[b][:, k, c4 * 128:(c4 + 1) * 128],
                            rhs=T.hT[:, k, th * 512:(th + 1) * 512], start=(k == 0), stop=(k == 7)) for k in range(8)],
                            reads=hreads + [("wgb", b)], writes=[("ps", pg)])
                        S.op("pe", [lambda e, k=k, b=b, c4=c4, th=th, pu=pu: e.matmul(
                            T.psum[:, pu, :], lhsT=wub[b][:, k, c4 * 128:(c4 + 1) * 128],
                            rhs=T.hT[:, k, th * 512:(th + 1) * 512], start=(k == 0), stop=(k == 7)) for k in range(8)],
                            reads=hreads + [("wub", b)], writes=[("ps", pu)])
                        S.op("act", lambda e, sgi=sgi, pg=pg: e.activation(out=sg[sgi][:], in_=T.psum[:, pg, :], func=AF.Silu),
                             reads=[("ps", pg)], writes=[("sg", sgi)])
                        S.op("dve", lambda e, sgi=sgi, pu=pu, c=c, th=th: e.tensor_tensor(
                            out=actT[:, c, th * 512:(th + 1) * 512], in0=sg[sgi][:], in1=T.psum[:, pu, :], op=ALU.mult),
                            reads=[("sg", sgi), ("ps", pu)], writes=[("actT", c, th)])
            for half in range(2):
                nd = (NCH + 3) // 4
                for dc in range(nd):
                    cc0 = dc * 4
                    ncc = min(4, NCH - cc0)
                    b = dcnt % 3
                    dcnt += 1
                    S.dma("pool", "wd%d" % b, lambda e, b=b, cc0=cc0, ncc=ncc, half=half: e.dma_start(
                        out=wdb[b][:, 0:ncc, :], in_=wd_v[:, cc0:cc0 + ncc, half * 512:(half + 1) * 512]),
                        writes=[("wdb", b)])
                    for j in range(8):
                        fns = []
                        for ci in range(ncc):
                            c = cc0 + ci
                            fns.append(lambda e, b=b, ci=ci, c=c, j=j: e.matmul(
                                T.psum[:, j, :], lhsT=actT[:, c, j * 128:(j + 1) * 128], rhs=wdb[b][:, ci, :],
                                start=(c == 0), stop=(c == NCH - 1)))
                        S.op("pe", fns, reads=[("wdb", b)] + [("actT", cc0 + ci, j // 4) for ci in range(ncc)],
                             writes=[("ps", j)])
                for j in range(8):
                    t = tiles[j]
                    S.op("dve", lambda e, j=j, t=t, half=half: e.scalar_tensor_tensor(
                        out=T.x[:, t, half * 512:(half + 1) * 512], in0=T.psum[:, j, :], scalar=final_scale,
                        in1=T.x[:, t, half * 512:(half + 1) * 512], op0=ALU.mult, op1=ALU.add),
                        reads=[("ps", j), ("x", t)], writes=[("x", t)])


_IDENT = np.eye(128, dtype=np.float32).astype(ml_dtypes.bfloat16)


def build_ffn_launch():
    C = Ctx()
    x_in = C.dram_in("x", [TOK, D], F32)
    wn = C.dram_in("wn", [D], F32)
    wg = C.dram_in("wg", [D, DFF], F32)
    wu = C.dram_in("wu", [D, DFF], F32)
    wd = C.dram_in("wd", [DFF, D], F32)
    ident = C.dram_in("ident", [128, 128], BF16)
    y = C.dram_out("y", [TOK, D], F32)
    with ExitStack() as es:
        T = TokPhase(C, es, ident)
        T.load_x(x_in)
        emit_ffn(C, es, T, wn, wg, wu, wd)
        T.store_x(y)
        C.close()
    return C


def run_ffn(x_shards, wn, wg, wu, wd, trace=False):
    C = build_ffn_launch()
    in_maps = [{"x": xs, "wn": wn, "wg": wg, "wu": wu, "wd": wd, "ident": _IDENT} for xs in x_shards]
    res = run_bass_kernel_spmd(C.nc, in_maps, core_ids=list(range(len(x_shards))), trace=trace)
    return [r["y"] for r in res.results], res


def emit_inproj(C, es0, T, wn_ap, w_ap, specs):
    S = C.S
    with ExitStack() as es:
        wb = [C.sb(es, "wpb", [128, 8, 512], BF16) for _ in range(2)]
        stg_b = [C.sb(es, "stg", [128, 512], BF16) for _ in range(3)]
        stg_f = [C.sb(es, "stgf", [128, 512], F32) for _ in range(3)]
        T.load_normw(wn_ap)
        w_v = w_ap.rearrange("(k p) n -> p k n", p=128)
        wcnt = 0
        pcnt = 0
        scnt = 0
        for g in range(TOK // 1024):
            tiles = list(range(g * 8, g * 8 + 8))
            T.norm_stats(tiles)
            T.norm_T(tiles)
            for (c0, w, mode, scale, dst) in specs:
                stg = stg_f if dst.dtype == F32 else stg_b
                skn = "stgf" if dst.dtype == F32 else "stg"
                b = wcnt % 2
                wcnt += 1
                S.dma("pool", "wp%d" % b, lambda e, b=b, c0=c0, w=w: e.dma_start(out=wb[b][:, :, 0:w], in_=w_v[:, :, c0:c0 + w]),
                      writes=[("wpb", b)])
                if mode == "T":
                    for cc in range(0, w, 128):
                        m = min(128, w - cc)
                        for th in range(2):
                            pb = pcnt % 4
                            pcnt += 1
                            si = scnt % 3
                            scnt += 1
                            hreads = [("hT", j) for j in range(th * 4, th * 4 + 4)]
                            S.op("pe", [lambda e, k=k, b=b, cc=cc, m=m, th=th, pb=pb: e.matmul(
                                T.psum[0:m, pb, :], lhsT=wb[b][:, k, cc:cc + m], rhs=T.hT[:, k, th * 512:(th + 1) * 512],
                                start=(k == 0), stop=(k == 7)) for k in range(8)],
                                reads=hreads + [("wpb", b)], writes=[("ps", pb)])
                            S.op("act", lambda e, si=si, pb=pb, m=m, scale=scale, stg=stg: e.activation(
                                out=stg[si][0:m, :], in_=T.psum[0:m, pb, :], func=AF.Identity, scale=float(scale)),
                                reads=[("ps", pb)], writes=[(skn, si)])
                            tok0 = g * 1024 + th * 512
                            S.dma("sp", "%s%d" % (skn, si), lambda e, si=si, m=m, cc=cc, tok0=tok0, dst=dst, stg=stg: e.dma_start(
                                out=dst[cc:cc + m, tok0:tok0 + 512], in_=stg[si][0:m, :]),
                                reads=[(skn, si)])
                else:
                    for j in range(8):
                        pb = pcnt % 4
                        pcnt += 1
                        si = scnt % 3
                        scnt += 1
                        S.op("pe", [lambda e, k=k, b=b, j=j, w=w, pb=pb: e.matmul(
                            T.psum[:, pb, 0:w], lhsT=T.hT[:, k, j * 128:(j + 1) * 128], rhs=wb[b][:, k, 0:w],
                            start=(k == 0), stop=(k == 7)) for k in range(8)],
                            reads=[("hT", j), ("wpb", b)], writes=[("ps", pb)])
                        S.op("act", lambda e, si=si, pb=pb, w=w, scale=scale, stg=stg: e.activation(
                            out=stg[si][:, 0:w], in_=T.psum[:, pb, 0:w], func=AF.Identity, scale=float(scale)),
                            reads=[("ps", pb)], writes=[(skn, si)])
                        tok0 = g * 1024 + j * 128
                        S.dma("sp", "%s%d" % (skn, si), lambda e, si=si, w=w, tok0=tok0, dst=dst, stg=stg: e.dma_start(
                            out=dst[tok0:tok0 + 128, 0:w], in_=stg[si][:, 0:w]),
                            reads=[(skn, si)])


def emit_outproj(C, es0, T, oT_ap, wo_ap):
    S = C.S
    with ExitStack() as es:
        wb = [C.sb(es, "wob", [128, 8, 512], BF16) for _ in range(2)]
        wo_v = wo_ap.rearrange("(k p) n -> p k n", p=128)
        for half in range(2):
            S.dma("pool", "wo%d" % half, lambda e, half=half: e.dma_start(out=wb[half][:], in_=wo_v[:, :, half * 512:(half + 1) * 512]),
                  writes=[("wob", half)])
        oT_v = oT_ap.rearrange("(k p) t -> p k t", p=128)
        pcnt = 0
        for g in range(TOK // 1024):
            S.dma("sp", "oTld", lambda e, g=g: e.dma_start(out=T.hT[:], in_=oT_v[:, :, g * 1024:(g + 1) * 1024]),
                  writes=[("hT", j) for j in range(8)])
            for j in range(8):
                t = g * 8 + j
                for half in range(2):
                    pb = pcnt % 4
                    pcnt += 1
                    S.op("pe", [lambda e, k=k, j=j, half=half, pb=pb: e.matmul(
                        T.psum[:, pb, :], lhsT=T.hT[:, k, j * 128:(j + 1) * 128], rhs=wb[half][:, k, :],
                        start=(k == 0), stop=(k == 7)) for k in range(8)],
                        reads=[("hT", j), ("wob", half)], writes=[("ps", pb)])
                    S.op("dve", lambda e, t=t, half=half, pb=pb: e.tensor_tensor(
                        out=T.x[:, t, half * 512:(half + 1) * 512], in0=T.x[:, t, half * 512:(half + 1) * 512],
                        in1=T.psum[:, pb, :], op=ALU.add),
                        reads=[("ps", pb), ("x", t)], writes=[("x", t)])


def emit_final_norm(C, es0, T, w_ap, out_ap):
    S = C.S
    with ExitStack() as es:
        yo = [C.sb(es, "yo", [128, D], F32) for _ in range(2)]
        T.load_normw(w_ap)
        tiles = list(range(NT))
        T.norm_stats(tiles)
        for t in tiles:
            b = t % 2
            S.op("dve", lambda e, t=t, b=b: e.scalar_tensor_tensor(
                out=yo[b][:], in0=T.x[:, t, :], scalar=T.rstd[:, t:t + 1], in1=T.wbc[:], op0=ALU.mult, op1=ALU.mult),
                reads=[("x", t), ("rstd", t), "wbc"], writes=[("yo", b)])
            S.dma("sp", "yo%d" % b, lambda e, t=t, b=b: e.dma_start(out=out_ap[t * 128:(t + 1) * 128, :], in_=yo[b][:]),
                  reads=[("yo", b)])


NEG = -30000.0
NQB = SEQ // 512
DBG_NDIFF = 2
DBG_NSB = 2
DBG_NQB = NQB


def att_constants(head_group):
    ki = np.arange(128, dtype=np.float64)[:, None]
    qi = np.arange(512, dtype=np.float64)[None, :]
    slopes = 2.0 ** (-8.0 * np.arange(1, 5) / 4)
    dbias = np.zeros((2, 5, 128, 512), np.float32)
    for hl in range(2):
        sl = slopes[2 * head_group + hl]
        dbias[hl, 4] = -sl * (qi - ki)
        for j in range(4):
            kk = 128 * j + ki
            allowed = (kk // 64) <= (qi // 64)
            dbias[hl, j] = np.where(allowed, -sl * np.abs(qi - kk), NEG)
    sbm = np.zeros((4, 128, 512), np.float32)
    for j in range(4):
        kk = 128 * j + ki
        sbm[j] = np.where(kk < qi, 0.0, NEG)
    ones = np.ones((128, 128), np.float32).astype(ml_dtypes.bfloat16)
    tri = (np.arange(128)[:, None] >= np.arange(128)[None, :]).astype(np.float32).astype(ml_dtypes.bfloat16)
    cbt = np.zeros((128, 64), np.float32)
    for hl in range(2):
        cbt[:, hl * 32:(hl + 1) * 32] = -slopes[2 * head_group + hl] * 128.0 * np.arange(32)[None, :]
    return dbias, sbm, ones, tri, cbt


def emit_attention(C, es0, psum, qaT, kaT, va, qsT, ksT, vs, lam_ap, subln_ap, dbias_ap, sbm_ap, ones_ap, tri_ap,
                   cbt_ap, oT, lambda_init):
    S = C.S
    with ExitStack() as es:
        qT = [C.sb(es, "qT", [128, SEQ], BF16) for _ in range(2)]
        kT = [C.sb(es, "kT", [128, SEQ], BF16) for _ in range(2)]
        V = [C.sb(es, "V", [128, SEQ // 128, 128], BF16) for _ in range(2)]
        dbias = C.sb(es, "dbias", [128, 10, 512], F32)
        sbm = C.sb(es, "sbm", [128, 4, 512], F32)
        ones = C.sb(es, "ones", [128, 128], BF16)
        tri = C.sb(es, "tri", [128, 128], BF16)
        cbt = C.sb(es, "cbt", [128, 64], F32)
        lamb = C.sb(es, "lamb", [128, 4, 64], F32)
        lprod = C.sb(es, "lprod", [128, 2, 64], F32)
        lsum = C.sb(es, "lsum", [128, 2], F32)
        neglam = C.sb(es, "neglam", [128, 1], F32)
        wsub = C.sb(es, "wsub", [128, 1], F32)
        tmp = [C.sb(es, "atmp", [128, 512], F32) for _ in range(3)]
        pT = [C.sb(es, "pT", [128, 512], BF16) for _ in range(3)]
        spt = [C.sb(es, "spt", [128, 512], BF16) for _ in range(2)]
        et = [C.sb(es, "et", [128, 512], F32) for _ in range(2)]
        zs = [C.sb(es, "zs", [128, 512], F32) for _ in range(2)]
        rrep = C.sb(es, "rrep", [128, 512], F32)
        o1 = C.sb(es, "o1", [128, 512], F32)
        o2 = C.sb(es, "o2", [128, 512], F32)
        rc = C.sb(es, "rc", [128, 512], F32)
        sqb = C.sb(es, "sqb", [128, 512], BF16)
        ostg = [C.sb(es, "ostg", [128, 512], BF16) for _ in range(2)]

        S.dma("sp", "c_dbias", lambda e: e.dma_start(out=dbias[:], in_=dbias_ap.rearrange("h j p q -> p (h j) q")), writes=["dbias"])
        S.dma("sp", "c_sbm", lambda e: e.dma_start(out=sbm[:], in_=sbm_ap.rearrange("j p q -> p j q")), writes=["sbm"])
        S.dma("sp", "c_ones", lambda e: e.dma_start(out=ones[:], in_=ones_ap), writes=["ones"])
        S.dma("sp", "c_tri", lambda e: e.dma_start(out=tri[:], in_=tri_ap), writes=["tri"])
        S.dma("sp", "c_cbt", lambda e: e.dma_start(out=cbt[:], in_=cbt_ap), writes=["cbt"])
        S.dma("sp", "c_lam", lambda e: e.dma_start(out=lamb[:].rearrange("p a b -> p (a b)"),
                                                   in_=bcast_rows(lam_ap.rearrange("a b -> (a b)"))), writes=["lamb"])
        S.dma("sp", "c_wsub", lambda e: e.dma_start(out=wsub[:], in_=subln_ap.rearrange("(p o) -> p o", o=1)), writes=["wsub"])
        S.op("dve", lambda e: e.tensor_tensor(out=lprod[:], in0=lamb[:, 0:4:2, :], in1=lamb[:, 1:4:2, :], op=ALU.mult),
             reads=["lamb"], writes=["lprod"])
        S.op("dve", lambda e: e.tensor_reduce(out=lsum[:], in_=lprod[:], axis=AX.X, op=ALU.add),
             reads=["lprod"], writes=["lsum"])
        S.op("act", lambda e: e.activation(out=lsum[:], in_=lsum[:], func=AF.Exp), reads=["lsum"], writes=["lsum"])
        S.op("dve", lambda e: e.tensor_tensor(out=neglam[:], in0=lsum[:, 1:2], in1=lsum[:, 0:1], op=ALU.subtract),
             reads=["lsum"], writes=["neglam"])
        S.op("dve", lambda e: e.tensor_scalar(out=neglam[:], in0=neglam[:], scalar1=-float(lambda_init), scalar2=None, op0=ALU.add),
             reads=["neglam"], writes=["neglam"])
        S.op("dve", lambda e: e.tensor_scalar(out=wsub[:], in0=wsub[:], scalar1=float(1.0 - lambda_init), scalar2=None, op0=ALU.mult),
             reads=["wsub"], writes=["wsub"])

        def load_head(buf, qsrc, ksrc, vsrc, hl):
            S.dma("sp", "ldq%d" % buf, lambda e: e.dma_start(out=qT[buf][:], in_=qsrc[hl * 128:(hl + 1) * 128, :]), writes=[("qT", buf)])
            S.dma("sp", "ldk%d" % buf, lambda e: e.dma_start(out=kT[buf][:], in_=ksrc[hl * 128:(hl + 1) * 128, :]), writes=[("kT", buf)])
            S.dma("sp", "ldv%d" % buf, lambda e: e.dma_start(
                out=V[buf][:], in_=vsrc[:, hl * 128:(hl + 1) * 128].rearrange("(kb p) d -> p kb d", p=128)), writes=[("V", buf)])

        cnt = {"s": 0, "t": 0, "p": 0, "o": 0}

        for hl in range(DBG_NDIFF):
            buf = hl
            load_head(buf, qaT, kaT, va, hl)
            for qb in range(DBG_NQB):
                nkb = 4 * qb + 4
                for c in range(2):
                    pO, pS = c, 2 + c
                    for kb in range(nkb):
                        ps_s = 4 + cnt["s"] % 2
                        cnt["s"] += 1
                        ti = cnt["t"] % 3
                        cnt["t"] += 1
                        S.op("pe", lambda e, c=c, kb=kb, qb=qb, ps_s=ps_s: e.matmul(
                            psum[:, ps_s, :], lhsT=kT[buf][c * 64:(c + 1) * 64, kb * 128:(kb + 1) * 128],
                            rhs=qT[buf][c * 64:(c + 1) * 64, qb * 512:(qb + 1) * 512], start=True, stop=True),
                            reads=[("qT", buf), ("kT", buf)], writes=[("ps", ps_s)])
                        j = kb - 4 * qb
                        bi = hl * 5 + (j if j >= 0 else 4)
                        S.op("dve", lambda e, ti=ti, ps_s=ps_s, bi=bi: e.tensor_tensor(
                            out=tmp[ti][:], in0=psum[:, ps_s, :], in1=dbias[:, bi, :], op=ALU.add),
                            reads=[("ps", ps_s), "dbias"], writes=[("tmp", ti)])
                        r = hl * 32 + (0 if j >= 0 else 4 * qb - kb)
                        S.op("act", lambda e, ti=ti, r=r: e.activation(out=pT[ti][:], in_=tmp[ti][:], func=AF.Exp, bias=cbt[:, r:r + 1]),
                             reads=[("tmp", ti), "cbt"], writes=[("pT", ti)])
                        S.op("pe", [lambda e, kb=kb, ti=ti, pO=pO, nkb=nkb: e.matmul(
                            psum[:, pO, :], lhsT=V[buf][:, kb, :], rhs=pT[ti][:], start=(kb == 0), stop=(kb == nkb - 1)),
                            lambda e, kb=kb, ti=ti, pS=pS, nkb=nkb: e.matmul(
                            psum[:, pS, :], lhsT=ones[:], rhs=pT[ti][:], start=(kb == 0), stop=(kb == nkb - 1))],
                            reads=[("V", buf), ("pT", ti), "ones"], writes=[("ps", pO), ("ps", pS)])
                S.op("dve", lambda e: e.reciprocal(out=rc[:], in_=psum[:, 2, :]), reads=[("ps", 2)], writes=["rc"])
                S.op("dve", lambda e: e.tensor_tensor(out=o1[:], in0=psum[:, 0, :], in1=rc[:], op=ALU.mult),
                     reads=[("ps", 0), "rc"], writes=["o1"])
                S.op("dve", lambda e: e.reciprocal(out=rc[:], in_=psum[:, 3, :]), reads=[("ps", 3)], writes=["rc"])
                S.op("dve", lambda e: e.tensor_tensor(out=o2[:], in0=psum[:, 1, :], in1=rc[:], op=ALU.mult),
                     reads=[("ps", 1), "rc"], writes=["o2"])
                S.op("dve", lambda e: e.scalar_tensor_tensor(out=o1[:], in0=o2[:], scalar=neglam[:, 0:1], in1=o1[:],
                                                             op0=ALU.mult, op1=ALU.add),
                     reads=["o1", "o2", "neglam"], writes=["o1"])
                S.op("act", lambda e: e.activation(out=sqb[:], in_=o1[:], func=AF.Square), reads=["o1"], writes=["sqb"])
                S.op("pe", lambda e: e.matmul(psum[:, 6, :], lhsT=ones[:], rhs=sqb[:], start=True, stop=True),
                     reads=["sqb", "ones"], writes=[("ps", 6)])
                S.op("act", lambda e: e.activation(out=rc[:], in_=psum[:, 6, :], func=AF.Ln, scale=1.0 / 128.0, bias=EPS),
                     reads=[("ps", 6)], writes=["rc"])
                S.op("act", lambda e: e.activation(out=rc[:], in_=rc[:], func=AF.Exp, scale=-0.5), reads=["rc"], writes=["rc"])
                oi = cnt["o"] % 2
                cnt["o"] += 1
                S.op("dve", lambda e, oi=oi: e.scalar_tensor_tensor(out=ostg[oi][:], in0=o1[:], scalar=wsub[:, 0:1], in1=rc[:],
                                                                    op0=ALU.mult, op1=ALU.mult),
                     reads=["o1", "rc", "wsub"], writes=[("ostg", oi)])
                S.dma("sp", "ost%d" % oi, lambda e, oi=oi, hl=hl, qb=qb: e.dma_start(
                    out=oT[hl * 128:(hl + 1) * 128, qb * 512:(qb + 1) * 512], in_=ostg[oi][:]), reads=[("ostg", oi)])

        for hl in range(DBG_NSB):
            buf = hl
            load_head(buf, qsT, ksT, vs, hl)
            for qb in range(DBG_NQB):
                nkb = 4 * qb + 4
                S.op("dve", lambda e: e.memset(rrep[:], 0.0), writes=["rrep"])
                for kb in range(nkb - 1, -1, -1):
                    ps_z = 4 + cnt["s"] % 2
                    ps_tri = 6 + cnt["s"] % 2
                    ps_cs = 2 + cnt["s"] % 2
                    zi = cnt["s"] % 2
                    cnt["s"] += 1
                    ti = cnt["t"] % 3
                    cnt["t"] += 1
                    S.op("pe", lambda e, kb=kb, qb=qb, ps_z=ps_z: e.matmul(
                        psum[:, ps_z, :], lhsT=kT[buf][:, kb * 128:(kb + 1) * 128], rhs=qT[buf][:, qb * 512:(qb + 1) * 512],
                        start=True, stop=True), reads=[("qT", buf), ("kT", buf)], writes=[("ps", ps_z)])
                    j = kb - 4 * qb
                    if j >= 0:
                        S.op("dve", lambda e, zi=zi, ps_z=ps_z, j=j: e.tensor_tensor(
                            out=zs[zi][:], in0=psum[:, ps_z, :], in1=sbm[:, j, :], op=ALU.add),
                            reads=[("ps", ps_z), "sbm"], writes=[("zs", zi)])
                        zsrc = zs[zi][:]
                        zkey = ("zs", zi)
                    else:
                        zsrc = psum[:, ps_z, :]
                        zkey = ("ps", ps_z)
                    S.op("act", lambda e, zi=zi, zsrc=zsrc: e.activation(out=et[zi][:], in_=zsrc, func=AF.Exp),
                         reads=[zkey], writes=[("et", zi)])
                    S.op("act", lambda e, zi=zi: e.activation(out=spt[zi][:], in_=et[zi][:], func=AF.Ln, bias=1.0),
                         reads=[("et", zi)], writes=[("spt", zi)])
                    S.op("pe", [lambda e, zi=zi, ps_tri=ps_tri: e.matmul(psum[:, ps_tri, :], lhsT=tri[:], rhs=spt[zi][:], start=True, stop=True),
                                lambda e, zi=zi, ps_cs=ps_cs: e.matmul(psum[:, ps_cs, :], lhsT=ones[:], rhs=spt[zi][:], start=True, stop=True)],
                         reads=[("spt", zi), "tri", "ones"], writes=[("ps", ps_tri), ("ps", ps_cs)])
                    S.op("dve", lambda e, ti=ti, zsrc=zsrc: e.tensor_tensor(out=tmp[ti][:], in0=zsrc, in1=rrep[:], op=ALU.subtract),
                         reads=[zkey, "rrep", ("et", zi)], writes=[("tmp", ti)])
                    S.op("dve", lambda e, ti=ti, ps_tri=ps_tri: e.tensor_tensor(out=tmp[ti][:], in0=tmp[ti][:], in1=psum[:, ps_tri, :], op=ALU.subtract),
                         reads=[("tmp", ti), ("ps", ps_tri)], writes=[("tmp", ti)])
                    S.op("act", lambda e, ti=ti: e.activation(out=pT[ti][:], in_=tmp[ti][:], func=AF.Exp),
                         reads=[("tmp", ti)], writes=[("pT", ti)])
                    S.op("pe", lambda e, kb=kb, ti=ti, nkb=nkb: e.matmul(
                        psum[:, 0, :], lhsT=V[buf][:, kb, :], rhs=pT[ti][:], start=(kb == nkb - 1), stop=(kb == 0)),
                        reads=[("V", buf), ("pT", ti)], writes=[("ps", 0)])
                    if kb > 0:
                        S.op("dve", lambda e, ps_cs=ps_cs: e.tensor_tensor(out=rrep[:], in0=rrep[:], in1=psum[:, ps_cs, :], op=ALU.add),
                             reads=["rrep", ("ps", ps_cs)], writes=["rrep"])
                oi = cnt["o"] % 2
                cnt["o"] += 1
                S.op("act", lambda e, oi=oi: e.copy(out=ostg[oi][:], in_=psum[:, 0, :]), reads=[("ps", 0)], writes=[("ostg", oi)])
                S.dma("sp", "ost%d" % oi, lambda e, oi=oi, hl=hl, qb=qb: e.dma_start(
                    out=oT[256 + hl * 128:256 + (hl + 1) * 128, qb * 512:(qb + 1) * 512], in_=ostg[oi][:]), reads=[("ostg", oi)])


def build_att_launch(lambda_init):
    C = Ctx()
    qaT = C.dram_in("qaT", [256, SEQ], BF16)
    kaT = C.dram_in("kaT", [256, SEQ], BF16)
    va = C.dram_in("va", [SEQ, 256], BF16)
    qsT = C.dram_in("qsT", [256, SEQ], BF16)
    ksT = C.dram_in("ksT", [256, SEQ], BF16)
    vs = C.dram_in("vs", [SEQ, 256], BF16)
    lam = C.dram_in("lam", [4, 64], F32)
    subln = C.dram_in("subln", [128], F32)
    dbias = C.dram_in("dbias", [2, 5, 128, 512], F32)
    sbm = C.dram_in("sbm", [4, 128, 512], F32)
    ones = C.dram_in("ones", [128, 128], BF16)
    tri = C.dram_in("tri", [128, 128], BF16)
    cbt = C.dram_in("cbt", [128, 64], F32)
    oT = C.dram_out("oT", [512, SEQ], BF16)
    with ExitStack() as es:
        psum = C.ps(es, "psum", [128, 8, 512], F32)
        emit_attention(C, es, psum, qaT, kaT, va, qsT, ksT, vs, lam, subln, dbias, sbm, ones, tri, cbt, oT, lambda_init)
        C.close()
    return C


GC = 128
GNC = SEQ // GC
DBG_GH = 4
DBG_GNC = GNC


def gdn_constants():
    s = np.arange(128)[:, None]
    c = np.arange(128)[None, :]
    cm = np.zeros((6, 128, 128), np.float32)
    cm[0] = np.eye(128)
    cm[1] = np.where(s <= c, 0.0, NEG)
    cm[2] = np.where(c <= s, 0.0, NEG)
    cm[3] = (s < c).astype(np.float32)
    cm[4] = (c < s).astype(np.float32)
    cm[5] = (s <= c).astype(np.float32)
    return cm


def emit_gdn(C, es0, psum, qkvT, gate, baT, batok, convw, alog, dtb, normw, cm_ap, oT):
    S = C.S
    NCH = DBG_GNC
    TT = NCH * GC
    with ExitStack() as es:
        cm = C.sb(es, "cm", [128, 6, 128], F32)
        ones = C.sb(es, "gones", [128, 128], F32)
        cw = C.sb(es, "cw", [128, 12, 4], F32)
        al = C.sb(es, "al", [128, 4], F32)
        db = C.sb(es, "db", [128, 4], F32)
        nwr = C.sb(es, "nwr", [128, 128], F32)
        bat = C.sb(es, "bat", [128, GNC, 8], F32)
        colA = C.sb(es, "colA", [128, 4, GNC], F32)
        colB = C.sb(es, "colB", [128, 4, GNC], F32)
        colBE = C.sb(es, "colBE", [128, 4, GNC], F32)
        colEL = C.sb(es, "colEL", [128, 4, GNC], F32)
        colCD = C.sb(es, "colCD", [128, 4, GNC], F32)
        xr = C.sb(es, "xr", [128, SEQ + 4], F32)
        acc = C.sb(es, "acc", [128, SEQ], F32)
        qT = C.sb(es, "gqT", [128, SEQ], F32)
        kT = C.sb(es, "gkT", [128, SEQ], F32)
        vT = C.sb(es, "gvT", [128, SEQ], F32)
        kbT = C.sb(es, "gkbT", [128, SEQ], F32)
        rA = C.sb(es, "rA", [128, SEQ], F32)
        rB = C.sb(es, "rB", [128, SEQ], F32)
        gcr = C.sb(es, "gcr", [128, SEQ], F32)
        oall = C.sb(es, "oall", [128, GNC, 128], F32)
        St = C.sb(es, "St", [128, 128], F32)
        kg = [C.sb(es, "kg", [128, 128], F32) for _ in range(2)]
        kd = [C.sb(es, "kd", [128, 128], F32) for _ in range(2)]
        vb = [C.sb(es, "vb", [128, 128], F32) for _ in range(2)]
        dT = [C.sb(es, "dT", [128, 128], F32) for _ in range(2)]
        dN = [C.sb(es, "dN", [128, 128], F32) for _ in range(2)]
        aq = [C.sb(es, "aq", [128, 128], F32) for _ in range(2)]
        P = [C.sb(es, "P", [128, 128], F32) for _ in range(2)]
        PT = [C.sb(es, "PT", [128, 128], F32) for _ in range(2)]
        Y = [C.sb(es, "Y", [128, 128], F32) for _ in range(2)]
        ut = [C.sb(es, "ut", [128, 128], F32) for _ in range(2)]
        wT = [C.sb(es, "wT", [128, 128], F32) for _ in range(2)]
        vn = [C.sb(es, "vn", [128, 128], F32) for _ in range(2)]
        ssq = C.sb(es, "ssq", [128, GNC], F32)
        ostg = [C.sb(es, "gostg", [128, 512], BF16) for _ in range(2)]
        ident = cm[:, 0, :]
        mU, mL, sU, sL, cumU = cm[:, 1, :], cm[:, 2, :], cm[:, 3, :], cm[:, 4, :], cm[:, 5, :]

        S.dma("sp", "g_cm", lambda e: e.dma_start(out=cm[:], in_=cm_ap.rearrange("k p q -> p k q")), writes=["cm"])
        S.dma("sp", "g_cw", lambda e: e.dma_start(out=cw[:], in_=convw), writes=["cw"])
        S.dma("sp", "g_al", lambda e: e.dma_start(out=al[:], in_=bcast_rows(alog)), writes=["al"])
        S.dma("sp", "g_db", lambda e: e.dma_start(out=db[:], in_=bcast_rows(dtb)), writes=["db"])
        S.dma("sp", "g_nw", lambda e: e.dma_start(out=nwr[:], in_=bcast_rows(normw)), writes=["nwr"])
        S.dma("sp", "g_bat", lambda e: e.dma_start(out=bat[:], in_=batok.rearrange("(n c) h -> c n h", c=128)), writes=["bat"])
        S.op("dve", lambda e: e.memset(ones[:], 1.0), writes=["gones"])
        S.op("dve", lambda e: e.memset(xr[:, 0:4], 0.0), writes=["xr0"])
        S.op("act", lambda e: e.activation(out=al[:], in_=al[:], func=AF.Exp), reads=["al"], writes=["al"])
        S.op("dve", lambda e: e.tensor_scalar(out=al[:], in0=al[:], scalar1=-1.0, scalar2=None, op0=ALU.mult), reads=["al"], writes=["al"])

        for h in range(DBG_GH):
            S.op("act", lambda e, h=h: e.activation(out=colB[:, h, :], in_=bat[:, :, h], func=AF.Sigmoid), reads=["bat"], writes=[("colB", h)])
            S.op("act", lambda e, h=h: e.activation(out=colA[:, h, :], in_=bat[:, :, 4 + h], func=AF.Exp, bias=db[:, h:h + 1]),
                 reads=["bat", "db"], writes=[("colA", h)])
            S.op("act", lambda e, h=h: e.activation(out=colA[:, h, :], in_=colA[:, h, :], func=AF.Ln, bias=1.0),
                 reads=[("colA", h)], writes=[("colA", h)])
            S.op("dve", lambda e, h=h: e.tensor_scalar(out=colA[:, h, :], in0=colA[:, h, :], scalar1=al[:, h:h + 1], scalar2=None, op0=ALU.mult),
                 reads=[("colA", h), "al"], writes=[("colA", h)])
            S.op("pe", lambda e, h=h: e.matmul(psum[:, 7, 0:GNC], lhsT=cumU, rhs=colA[:, h, :], start=True, stop=True),
                 reads=[("colA", h), "cm"], writes=[("ps", 7)])
            S.op("dve", lambda e, h=h: e.tensor_copy(out=colA[:, h, :], in_=psum[:, 7, 0:GNC]), reads=[("ps", 7)], writes=[("colA", h)])
            S.op("act", lambda e, h=h: e.activation(out=colBE[:, h, :], in_=colA[:, h, :], func=AF.Exp), reads=[("colA", h)], writes=[("colBE", h)])
            S.op("dve", lambda e, h=h: e.tensor_tensor(out=colBE[:, h, :], in0=colBE[:, h, :], in1=colB[:, h, :], op=ALU.mult),
                 reads=[("colBE", h), ("colB", h)], writes=[("colBE", h)])

        nq = TT // 512
        for h in range(DBG_GH):
            for qi, dst in enumerate((qT, kT, vT)):
                row0 = qi * 512 + h * 128
                S.dma("sp", "g_xr", lambda e, row0=row0: e.dma_start(out=xr[:, 4:4 + SEQ], in_=qkvT[row0:row0 + 128, :]),
                      reads=["xr0"], writes=["xr"])
                wi = qi * 4 + h
                S.op("dve", lambda e, wi=wi: e.tensor_scalar(out=acc[:, 0:TT], in0=xr[:, 1:1 + TT], scalar1=cw[:, wi, 0:1], scalar2=None, op0=ALU.mult),
                     reads=["xr", "cw"], writes=["acc"])
                for tap in range(1, 4):
                    S.op("dve", lambda e, wi=wi, tap=tap: e.scalar_tensor_tensor(
                        out=acc[:, 0:TT], in0=xr[:, 1 + tap:1 + tap + TT], scalar=cw[:, wi, tap:tap + 1], in1=acc[:, 0:TT],
                        op0=ALU.mult, op1=ALU.add), reads=["xr", "cw", "acc"], writes=["acc"])
                S.op("act", lambda e, dst=dst: e.activation(out=dst[:, 0:TT], in_=acc[:, 0:TT], func=AF.Silu), reads=["acc"], writes=[("f", qi)])
                if qi < 2:
                    for j in range(nq):
                        sl = slice(j * 512, (j + 1) * 512)
                        S.op("act", lambda e, dst=dst, sl=sl: e.activation(out=acc[:, sl], in_=dst[:, sl], func=AF.Square),
                             reads=[("f", qi)], writes=["acc"])
                        pb = j % 2
                        S.op("pe", lambda e, sl=sl, pb=pb: e.matmul(psum[:, pb, :], lhsT=ones[:], rhs=acc[:, sl], start=True, stop=True),
                             reads=["acc", "gones"], writes=[("ps", pb)])
                        S.op("act", lambda e, sl=sl, pb=pb: e.activation(out=acc[:, sl], in_=psum[:, pb, :], func=AF.Ln, bias=1e-6),
                             reads=[("ps", pb)], writes=["acc"])
                        S.op("act", lambda e, sl=sl: e.activation(out=acc[:, sl], in_=acc[:, sl], func=AF.Exp, scale=-0.5),
                             reads=["acc"], writes=["acc"])
                        sc = (128.0 ** -0.5) if qi == 0 else 1.0
                        S.op("dve", lambda e, dst=dst, sl=sl, sc=sc: e.scalar_tensor_tensor(
                            out=dst[:, sl], in0=dst[:, sl], scalar=float(sc), in1=acc[:, sl], op0=ALU.mult, op1=ALU.mult),
                            reads=[("f", qi), "acc"], writes=[("f", qi)])
            S.dma("sp", "g_rA", lambda e, h=h: e.dma_start(out=rA[:], in_=baT[h:h + 1, :].broadcast_to([128, SEQ])), writes=["rA"])
            S.op("act", lambda e: e.activation(out=rA[:, 0:TT], in_=rA[:, 0:TT], func=AF.Sigmoid), reads=["rA"], writes=["rA"])
            S.op("dve", lambda e: e.tensor_tensor(out=kbT[:, 0:TT], in0=kT[:, 0:TT], in1=rA[:, 0:TT], op=ALU.mult),
                 reads=["rA", ("f", 1)], writes=["kbT"])
            S.dma("sp", "g_rB", lambda e, h=h: e.dma_start(out=rB[:], in_=baT[4 + h:5 + h, :].broadcast_to([128, SEQ])), writes=["rB"])
            S.op("act", lambda e, h=h: e.activation(out=rB[:, 0:TT], in_=rB[:, 0:TT], func=AF.Exp, bias=db[:, h:h + 1]), reads=["rB", "db"], writes=["rB"])
            S.op("act", lambda e: e.activation(out=rB[:, 0:TT], in_=rB[:, 0:TT], func=AF.Ln, bias=1.0), reads=["rB"], writes=["rB"])
            S.op("dve", lambda e, h=h: e.tensor_scalar(out=rB[:, 0:TT], in0=rB[:, 0:TT], scalar1=al[:, h:h + 1], scalar2=None, op0=ALU.mult),
                 reads=["rB", "al"], writes=["rB"])
            src, dstb, skey, dkey = rB, rA, "rB", "rA"
            d = 1
            while d < GC:
                sv = src[:, 0:TT].rearrange("p (n c) -> p n c", c=GC)
                dv = dstb[:, 0:TT].rearrange("p (n c) -> p n c", c=GC)
                S.op("dve", lambda e, sv=sv, dv=dv, d=d: e.tensor_tensor(out=dv[:, :, d:GC], in0=sv[:, :, d:GC], in1=sv[:, :, 0:GC - d], op=ALU.add),
                     reads=[skey], writes=[dkey])
                S.op("act", lambda e, sv=sv, dv=dv, d=d: e.copy(out=dv[:, :, 0:d], in_=sv[:, :, 0:d]), reads=[skey, dkey], writes=[dkey])
                src, dstb, skey, dkey = dstb, src, dkey, skey
                d *= 2
            S.op("dve", lambda e, src=src: e.tensor_copy(out=gcr[:, 0:TT], in_=src[:, 0:TT]), reads=[skey, dkey], writes=["gcr"])
            S.op("act", lambda e: e.activation(out=rA[:, 0:TT], in_=gcr[:, 0:TT], func=AF.Exp), reads=["gcr", "rB"], writes=["rA"])
            S.op("dve", lambda e: e.tensor_tensor(out=acc[:, 0:TT], in0=qT[:, 0:TT], in1=rA[:, 0:TT], op=ALU.mult),
                 reads=["rA", ("f", 0), "acc"], writes=["acc"])
            qgT = acc
            gl = gcr[:, 0:TT].rearrange("p (n c) -> p n c", c=GC)[:, :, GC - 1]
            S.op("dve", lambda e, h=h, gl=gl: e.tensor_tensor(out=colEL[:, h, 0:NCH], in0=gl, in1=colA[:, h, 0:NCH], op=ALU.subtract),
                 reads=["gcr", ("colA", h)], writes=[("colEL", h)])
            S.op("act", lambda e, h=h: e.activation(out=colEL[:, h, 0:NCH], in_=colEL[:, h, 0:NCH], func=AF.Exp), reads=[("colEL", h)], writes=[("colEL", h)])
            S.op("act", lambda e, h=h, gl=gl: e.activation(out=colCD[:, h, 0:NCH], in_=gl, func=AF.Exp), reads=["gcr"], writes=[("colCD", h)])
            S.op("dve", lambda e: e.memset(St[:], 0.0), writes=["St"])

            for n in range(NCH):
                b = n % 2
                cs = slice(n * GC, (n + 1) * GC)
                S.op("pe", [lambda e, cs=cs: e.transpose(out=psum[:, 0, 0:128], in_=kT[:, cs], identity=ident),
                            lambda e, cs=cs: e.transpose(out=psum[:, 0, 128:256], in_=vT[:, cs], identity=ident)],
                     reads=[("f", 1), ("f", 2), "cm"], writes=[("ps", 0)])
                S.op("dve", lambda e, b=b, h=h, n=n: e.tensor_scalar(out=kg[b][:], in0=psum[:, 0, 0:128], scalar1=colBE[:, h, n:n + 1], scalar2=None, op0=ALU.mult),
                     reads=[("ps", 0), ("colBE", h)], writes=[("kg", b)])
                S.op("dve", lambda e, b=b, h=h, n=n: e.tensor_scalar(out=kd[b][:], in0=psum[:, 0, 0:128], scalar1=colEL[:, h, n:n + 1], scalar2=None, op0=ALU.mult),
                     reads=[("ps", 0), ("colEL", h)], writes=[("kd", b)])
                S.op("dve", lambda e, b=b, h=h, n=n: e.tensor_scalar(out=vb[b][:], in0=psum[:, 0, 128:256], scalar1=colB[:, h, n:n + 1], scalar2=None, op0=ALU.mult),
                     reads=[("ps", 0), ("colB", h)], writes=[("vb", b)])
                S.op("dve", lambda e, b=b, h=h, n=n, cs=cs: e.scalar_tensor_tensor(
                    out=dT[b][:], in0=gcr[:, cs], scalar=colA[:, h, n:n + 1], in1=mU, op0=ALU.subtract, op1=ALU.min),
                    reads=["gcr", ("colA", h), "cm"], writes=[("dT", b)])
                S.op("act", lambda e, b=b: e.activation(out=dT[b][:], in_=dT[b][:], func=AF.Exp), reads=[("dT", b)], writes=[("dT", b)])
                S.op("dve", lambda e, b=b, h=h, n=n, cs=cs: e.tensor_scalar(
                    out=dN[b][:], in0=gcr[:, cs], scalar1=colA[:, h, n:n + 1], scalar2=-1.0, op0=ALU.subtract, op1=ALU.mult),
                    reads=["gcr", ("colA", h)], writes=[("dN", b)])
                S.op("dve", lambda e, b=b: e.tensor_tensor(out=dN[b][:], in0=dN[b][:], in1=mL, op=ALU.min), reads=[("dN", b), "cm"], writes=[("dN", b)])
                S.op("act", lambda e, b=b: e.activation(out=dN[b][:], in_=dN[b][:], func=AF.Exp), reads=[("dN", b)], writes=[("dN", b)])
                S.op("pe", [lambda e, cs=cs: e.matmul(psum[:, 1, 0:128], lhsT=kT[:, cs], rhs=kbT[:, cs], start=True, stop=True),
                            lambda e, cs=cs: e.matmul(psum[:, 1, 128:256], lhsT=kbT[:, cs], rhs=kT[:, cs], start=True, stop=True),
                            lambda e, cs=cs: e.matmul(psum[:, 1, 256:384], lhsT=kT[:, cs], rhs=qT[:, cs], start=True, stop=True)],
                     reads=[("f", 0), ("f", 1), "kbT"], writes=[("ps", 1)])
                S.op("dve", lambda e, b=b: e.tensor_tensor(out=P[0][:], in0=psum[:, 1, 0:128], in1=dT[b][:], op=ALU.mult),
                     reads=[("ps", 1), ("dT", b)], writes=[("P", 0)])
                S.op("dve", lambda e: e.tensor_tensor(out=P[0][:], in0=P[0][:], in1=sU, op=ALU.mult), reads=[("P", 0), "cm"], writes=[("P", 0)])
                S.op("dve", lambda e, b=b: e.tensor_tensor(out=PT[0][:], in0=psum[:, 1, 128:256], in1=dN[b][:], op=ALU.mult),
                     reads=[("ps", 1), ("dN", b)], writes=[("PT", 0)])
                S.op("dve", lambda e: e.tensor_tensor(out=PT[0][:], in0=PT[0][:], in1=sL, op=ALU.mult), reads=[("PT", 0), "cm"], writes=[("PT", 0)])
                S.op("dve", lambda e, b=b: e.tensor_tensor(out=aq[b][:], in0=psum[:, 1, 256:384], in1=dT[b][:], op=ALU.mult),
                     reads=[("ps", 1), ("dT", b)], writes=[("aq", b)])
                S.op("dve", lambda e: e.tensor_tensor(out=Y[0][:], in0=ident, in1=P[0][:], op=ALU.subtract), reads=[("P", 0), "cm"], writes=[("Y", 0)])
                cur = 0
                for lev in range(1, 7):
                    nxt = 1 - cur
                    last = (lev == 6)
                    fns = [lambda e, cur=cur: e.matmul(psum[:, 2, 128:256], lhsT=P[cur][:], rhs=PT[cur][:], start=True, stop=True)]
                    if not last:
                        fns.append(lambda e, cur=cur: e.matmul(psum[:, 2, 0:128], lhsT=PT[cur][:], rhs=P[cur][:], start=True, stop=True))
                    S.op("pe", fns, reads=[("P", cur), ("PT", cur)], writes=[("ps", 2)])
                    S.op("act", lambda e, nxt=nxt: e.copy(out=PT[nxt][:], in_=psum[:, 2, 128:256]), reads=[("ps", 2)], writes=[("PT", nxt)])
                    if not last:
                        S.op("act", lambda e, nxt=nxt: e.copy(out=P[nxt][:], in_=psum[:, 2, 0:128]), reads=[("ps", 2)], writes=[("P", nxt)])
                    S.op("pe", lambda e, nxt=nxt, cur=cur: e.matmul(psum[:, 3, 0:128], lhsT=PT[nxt][:], rhs=Y[cur][:], start=True, stop=True),
                         reads=[("PT", nxt), ("Y", cur)], writes=[("ps", 3)])
                    S.op("dve", lambda e, nxt=nxt, cur=cur: e.tensor_tensor(out=Y[nxt][:], in0=Y[cur][:], in1=psum[:, 3, 0:128], op=ALU.add),
                         reads=[("ps", 3), ("Y", cur)], writes=[("Y", nxt)])
                    cur = nxt
                TTm = Y[cur]
                tkey = ("Y", cur)
                S.op("pe", [lambda e, b=b, TTm=TTm: e.matmul(psum[:, 4, 0:128], lhsT=TTm[:], rhs=vb[b][:], start=True, stop=True),
                            lambda e, b=b, TTm=TTm: e.matmul(psum[:, 4, 128:256], lhsT=kg[b][:], rhs=TTm[:], start=True, stop=True)],
                     reads=[tkey, ("vb", b), ("kg", b)], writes=[("ps", 4)])
                S.op("act", lambda e, b=b: e.copy(out=ut[b][:], in_=psum[:, 4, 0:128]), reads=[("ps", 4)], writes=[("ut", b)])
                S.op("act", lambda e, b=b: e.copy(out=wT[b][:], in_=psum[:, 4, 128:256]), reads=[("ps", 4)], writes=[("wT", b)])
                S.op("pe", lambda e, b=b: e.matmul(psum[:, 5, 0:128], lhsT=wT[b][:], rhs=St[:], start=True, stop=True),
                     reads=[("wT", b), "St"], writes=[("ps", 5)])
                S.op("dve", lambda e, b=b: e.tensor_tensor(out=vn[b][:], in0=ut[b][:], in1=psum[:, 5, 0:128], op=ALU.subtract),
                     reads=[("ut", b), ("ps", 5)], writes=[("vn", b)])
                S.op("pe", [lambda e, cs=cs: e.matmul(psum[:, 6, 0:128], lhsT=qgT[:, cs], rhs=St[:], start=True, stop=False),
                            lambda e, b=b: e.matmul(psum[:, 6, 0:128], lhsT=aq[b][:], rhs=vn[b][:], start=False, stop=True),
                            lambda e, b=b: e.matmul(psum[:, 6, 128:256], lhsT=kd[b][:], rhs=vn[b][:], start=True, stop=True)],
                     reads=["acc", "St", ("aq", b), ("vn", b), ("kd", b)], writes=[("ps", 6)])
                S.op("act", lambda e, n=n: e.copy(out=oall[:, n, :], in_=psum[:, 6, 0:128]), reads=[("ps", 6)], writes=[("oall", n)])
                S.op("dve", lambda e, h=h, n=n: e.scalar_tensor_tensor(
                    out=St[:], in0=St[:], scalar=colCD[:, h, n:n + 1], in1=psum[:, 6, 128:256], op0=ALU.mult, op1=ALU.add),
                    reads=["St", ("colCD", h), ("ps", 6), ("oall", n)], writes=["St"])

            okeys = [("oall", n) for n in range(NCH)]
            S.dma("sp", "g_gate", lambda e, h=h: e.dma_start(
                out=rB[:].rearrange("p (n d) -> p n d", d=128), in_=gate[:, h * 128:(h + 1) * 128].rearrange("(n c) d -> c n d", c=128)),
                reads=["rA", "gcr"], writes=["rB"])
            S.op("act", lambda e: e.activation(out=rB[:, 0:TT], in_=rB[:, 0:TT], func=AF.Silu), reads=["rB"], writes=["rB"])
            S.op("dve", lambda e: e.tensor_tensor(out=rB[:, 0:TT].rearrange("p (n d) -> p n d", d=128),
                                                  in0=rB[:, 0:TT].rearrange("p (n d) -> p n d", d=128),
                                                  in1=nwr[:].unsqueeze(1).broadcast_to([128, NCH, 128]), op=ALU.mult),
                 reads=["rB", "nwr"], writes=["rB"])
            for n in range(NCH):
                S.op("act", lambda e, n=n: e.activation(out=rA[:, n * 128:(n + 1) * 128], in_=oall[:, n, :], func=AF.Square,
                                                        scale=float(128.0 ** -0.5), accum_out=ssq[:, n:n + 1]),
                     reads=[("oall", n), "rA"], writes=["rA", ("ssq", n)])
            S.op("act", lambda e: e.activation(out=ssq[:, 0:NCH], in_=ssq[:, 0:NCH], func=AF.Ln, bias=1e-6),
                 reads=[("ssq", n) for n in range(NCH)], writes=["ssq"])
            S.op("act", lambda e: e.activation(out=ssq[:, 0:NCH], in_=ssq[:, 0:NCH], func=AF.Exp, scale=-0.5), reads=["ssq"], writes=["ssq"])
            for n in range(NCH):
                S.op("dve", lambda e, n=n: e.scalar_tensor_tensor(
                    out=oall[:, n, :], in0=oall[:, n, :], scalar=ssq[:, n:n + 1], in1=rB[:, n * 128:(n + 1) * 128], op0=ALU.mult, op1=ALU.mult),
                    reads=[("oall", n), "ssq", "rB"], writes=[("oall", n)])
            for j in range(NCH // 4):
                pb = j % 2
                S.op("pe", [lambda e, j=j, i=i, pb=pb: e.transpose(out=psum[:, pb, i * 128:(i + 1) * 128], in_=oall[:, j * 4 + i, :], identity=ident)
                            for i in range(4)], reads=[("oall", j * 4 + i) for i in range(4)] + ["cm"], writes=[("ps", pb)])
                S.op("act", lambda e, pb=pb: e.copy(out=ostg[pb][:], in_=psum[:, pb, :]), reads=[("ps", pb)], writes=[("gostg", pb)])
                S.dma("sp", "g_ost%d" % pb, lambda e, pb=pb, j=j, h=h: e.dma_start(
                    out=oT[h * 128:(h + 1) * 128, j * 512:(j + 1) * 512], in_=ostg[pb][:]), reads=[("gostg", pb)])


def build_gdn_launch():
    C = Ctx()
    qkvT = C.dram_in("qkvT", [1536, SEQ], F32)
    gate = C.dram_in("gate", [SEQ, 512], F32)
    baT = C.dram_in("baT", [8, SEQ], F32)
    batok = C.dram_in("batok", [SEQ, 8], F32)
    convw = C.dram_in("convw", [128, 12, 4], F32)
    alog = C.dram_in("alog", [4], F32)
    dtb = C.dram_in("dtb", [4], F32)
    normw = C.dram_in("normw", [128], F32)
    cm = C.dram_in("cm", [6, 128, 128], F32)
    oT = C.dram_out("oT", [512, SEQ], BF16)
    with ExitStack() as es:
        psum = C.ps(es, "psum", [128, 8, 512], F32)
        emit_gdn(C, es, psum, qkvT, gate, baT, batok, convw, alog, dtb, normw, cm, oT)
        C.close()
    return C


import math

_BF = ml_dtypes.bfloat16


def build_A(even):
    C = Ctx()
    x_in = C.dram_in("x", [TOK, D], F32)
    ident = C.dram_in("ident", [128, 128], BF16)
    wn1 = C.dram_in("wn1", [D], F32)
    wg = C.dram_in("wg", [D, DFF], F32)
    wu = C.dram_in("wu", [D, DFF], F32)
    wd = C.dram_in("wd", [DFF, D], F32)
    wnm = C.dram_in("wnm", [D], F32)
    y = C.dram_out("y", [TOK, D], F32)
    if even:
        win = C.dram_in("win", [D, 3072], F32)
        featT = C.dram_out("featT", [2048, TOK], BF16)
        vtok = C.dram_out("vtok", [TOK, 1024], BF16)
        specs = [(0, 512, "T", 0.125, featT[0:512, :]), (512, 512, "T", 1.0, featT[512:1024, :]),
                 (1024, 512, "tok", 1.0, vtok[:, 0:512]), (1536, 512, "T", 128.0 ** -0.5, featT[1024:1536, :]),
                 (2048, 512, "T", 1.0, featT[1536:2048, :]), (2560, 512, "tok", 1.0, vtok[:, 512:1024])]
    else:
        win = C.dram_in("win", [D, 4112], F32)
        qkvT = C.dram_out("qkvT", [3072, TOK], F32)
        gate = C.dram_out("gate", [TOK, 1024], F32)
        baT = C.dram_out("baT", [16, TOK], F32)
        batok = C.dram_out("batok", [TOK, 16], F32)
        specs = [(c0, 512, "T", 1.0, qkvT[c0:c0 + 512, :]) for c0 in range(0, 3072, 512)]
        specs += [(3072, 512, "tok", 1.0, gate[:, 0:512]), (3584, 512, "tok", 1.0, gate[:, 512:1024]),
                  (4096, 16, "T", 1.0, baT), (4096, 16, "tok", 1.0, batok)]
    with ExitStack() as es:
        T = TokPhase(C, es, ident)
        T.load_x(x_in)
        emit_ffn(C, es, T, wn1, wg, wu, wd)
        C.S.barrier()
        T.store_x(y)
        emit_inproj(C, es, T, wnm, win, specs)
        C.close()
    return C


def build_C(last):
    C = Ctx()
    x_in = C.dram_in("x", [TOK, D], F32)
    ident = C.dram_in("ident", [128, 128], BF16)
    oT = C.dram_in("oT", [1024, TOK], BF16)
    wo = C.dram_in("wo", [1024, D], F32)
    wn2 = C.dram_in("wn2", [D], F32)
    wg = C.dram_in("wg", [D, DFF], F32)
    wu = C.dram_in("wu", [D, DFF], F32)
    wd = C.dram_in("wd", [DFF, D], F32)
    if last:
        wf = C.dram_in("wf", [D], F32)
    y = C.dram_out("y", [TOK, D], F32)
    with ExitStack() as es:
        T = TokPhase(C, es, ident)
        T.load_x(x_in)
        emit_outproj(C, es, T, oT, wo)
        C.S.barrier()
        emit_ffn(C, es, T, wn2, wg, wu, wd)
        C.S.barrier()
        if last:
            emit_final_norm(C, es, T, wf, y)
        else:
            T.store_x(y)
        C.close()
    return C


def _run(C, in_maps):
    res = run_bass_kernel_spmd(C.nc, in_maps, core_ids=list(range(NCORES)))
    return res.results


def kernel(x, ffn1_norm, ffn1_w_gate, ffn1_w_up, ffn1_w_down, mix_norm, att_w_in, diff_lambda, diff_subln, att_w_out,
           gdn_w_in, gdn_conv_w, gdn_a_log, gdn_dt_bias, gdn_norm, gdn_w_out, ffn2_norm, ffn2_w_gate, ffn2_w_up,
           ffn2_w_down, final_norm):
    f = lambda a: np.ascontiguousarray(np.asarray(a, dtype=np.float32))
    x = f(x)
    xs = [np.ascontiguousarray(x[c // 2, (c % 2) * TOK:(c % 2 + 1) * TOK, :]) for c in range(NCORES)]
    cat_t = lambda b, fn: np.ascontiguousarray(np.concatenate([fn(2 * b), fn(2 * b + 1)], axis=1))
    cat_r = lambda b, fn: np.ascontiguousarray(np.concatenate([fn(2 * b), fn(2 * b + 1)], axis=0))
    for l in range(DEPTH):
        even = (l % 2 == 0)
        li = l // 2
        CA = build_A(even)
        win = f(att_w_in[li]) if even else f(gdn_w_in[li])
        rA = _run(CA, [{"x": xs[c], "ident": _IDENT, "wn1": f(ffn1_norm[l]), "wg": f(ffn1_w_gate[l]), "wu": f(ffn1_w_up[l]),
                        "wd": f(ffn1_w_down[l]), "wnm": f(mix_norm[l]), "win": win} for c in range(NCORES)])
        xs = [rA[c]["y"] for c in range(NCORES)]
        maps = []
        if even:
            lambda_init = 0.8 - 0.6 * math.exp(-0.3 * l)
            CB = build_att_launch(lambda_init)
            for c in range(NCORES):
                b, hg = c // 2, c % 2
                dbias, sbm, ones, tri, cbt = att_constants(hg)
                r0 = hg * 256
                maps.append({
                    "qaT": cat_t(b, lambda cc: rA[cc]["featT"][r0:r0 + 256]),
                    "kaT": cat_t(b, lambda cc: rA[cc]["featT"][512 + r0:512 + r0 + 256]),
                    "qsT": cat_t(b, lambda cc: rA[cc]["featT"][1024 + r0:1024 + r0 + 256]),
                    "ksT": cat_t(b, lambda cc: rA[cc]["featT"][1536 + r0:1536 + r0 + 256]),
                    "va": cat_r(b, lambda cc: rA[cc]["vtok"][:, r0:r0 + 256]),
                    "vs": cat_r(b, lambda cc: rA[cc]["vtok"][:, 512 + r0:512 + r0 + 256]),
                    "lam": f(diff_lambda[li]), "subln": f(diff_subln[li]), "dbias": dbias, "sbm": sbm, "ones": ones,
                    "tri": tri, "cbt": cbt})
            rB = _run(CB, maps)
            oTf = []
            for c in range(NCORES):
                b, half = c // 2, c % 2
                ts = slice(half * TOK, (half + 1) * TOK)
                o0, o1 = rB[2 * b]["oT"], rB[2 * b + 1]["oT"]
                oTf.append(np.ascontiguousarray(np.concatenate([o0[0:256, ts], o1[0:256, ts], o0[256:512, ts], o1[256:512, ts]], axis=0)))
            wo = f(att_w_out[li])
        else:
            CB = build_gdn_launch()
            cwl = f(gdn_conv_w[li])
            cmc = gdn_constants()
            for c in range(NCORES):
                b, hg = c // 2, c % 2
                r0 = hg * 512
                cw = np.stack([cwl[:, qi * 1024 + r0:qi * 1024 + r0 + 512] for qi in range(3)], 0)
                cw = np.ascontiguousarray(cw.reshape(3, 4, 4, 128).transpose(3, 0, 2, 1).reshape(128, 12, 4))
                maps.append({
                    "qkvT": np.ascontiguousarray(np.concatenate(
                        [cat_t(b, lambda cc, qi=qi: rA[cc]["qkvT"][qi * 1024 + r0:qi * 1024 + r0 + 512]) for qi in range(3)], axis=0)),
                    "gate": cat_r(b, lambda cc: rA[cc]["gate"][:, r0:r0 + 512]),
                    "baT": np.ascontiguousarray(np.concatenate([cat_t(b, lambda cc: rA[cc]["baT"][hg * 4:hg * 4 + 4]),
                                                                cat_t(b, lambda cc: rA[cc]["baT"][8 + hg * 4:8 + hg * 4 + 4])], axis=0)),
                    "batok": np.ascontiguousarray(np.concatenate([cat_r(b, lambda cc: rA[cc]["batok"][:, hg * 4:hg * 4 + 4]),
                                                                  cat_r(b, lambda cc: rA[cc]["batok"][:, 8 + hg * 4:8 + hg * 4 + 4])], axis=1)),
                    "convw": cw, "alog": f(gdn_a_log[li])[hg * 4:hg * 4 + 4].copy(), "dtb": f(gdn_dt_bias[li])[hg * 4:hg * 4 + 4].copy(),
                    "normw": f(gdn_norm[li]), "cm": cmc})
            rB = _run(CB, maps)
            oTf = []
            for c in range(NCORES):
                b, half = c // 2, c % 2
                ts = slice(half * TOK, (half + 1) * TOK)
                oTf.append(np.ascontiguousarray(np.concatenate([rB[2 * b]["oT"][:, ts], rB[2 * b + 1]["oT"][:, ts]], axis=0)))
            wo = f(gdn_w_out[li])
        last = (l == DEPTH - 1)
        CC = build_C(last)
        mC = []
        for c in range(NCORES):
            m = {"x": xs[c], "ident": _IDENT, "oT": oTf[c], "wo": wo, "wn2": f(ffn2_norm[l]), "wg": f(ffn2_w_gate[l]),
                 "wu": f(ffn2_w_up[l]), "wd": f(ffn2_w_down[l])}
            if last:
                m["wf"] = f(final_norm)
            mC.append(m)
        rC = _run(CC, mC)
        xs = [rC[c]["y"] for c in range(NCORES)]
    out = np.zeros((NB, SEQ, D), np.float32)
    for c in range(NCORES):
        out[c // 2, (c % 2) * TOK:(c % 2 + 1) * TOK, :] = xs[c]
    return out
```
